# Optimizing a Trainium2 kernel written in Bass

```python
import jax
import jax.numpy as jnp
from jax import lax
import numpy as np

D_MODEL = 4096
BATCH = 2
SEQ = 8192
DEPTH = 2

HEAD_DIM = 128
MIX_HEADS = D_MODEL // HEAD_DIM
QBLOCK = 128
ROPE_THETA = 10000.0
LN_EPS = 1e-5
RMS_EPS = 1e-6

SB_HEADS = MIX_HEADS // 4
DIL_HEADS = MIX_HEADS // 4
DIL_BRANCHES = ((128, 1), (512, 4), (2048, 16))
MLA_HEADS = MIX_HEADS // 4
MLA_Q_RANK = D_MODEL // 4
MLA_KV_RANK = 512
MLA_NOPE = 128
MLA_ROPE = 64
MLA_V = 128
DIFF_HEADS = MIX_HEADS // 8
DIFF_DIM = HEAD_DIM
D_FF = 4 * D_MODEL
PLE_DIM = 256
DN_ALPHA = (2 * DEPTH) ** 0.25
DN_BETA = (8 * DEPTH) ** -0.25

IN_WIDTHS = (SB_HEADS * HEAD_DIM,) * 3 + (DIL_HEADS * HEAD_DIM,) * 3 + (MLA_Q_RANK, MLA_KV_RANK, MLA_ROPE) + (DIFF_HEADS * 2 * DIFF_DIM,) * 3
N_IN = sum(IN_WIDTHS)
MIX_OUT = SB_HEADS * HEAD_DIM + DIL_HEADS * HEAD_DIM + MLA_HEADS * MLA_V + DIFF_HEADS * 2 * DIFF_DIM

kernel_name = 'hybrid_parallel_heads_deepnorm'


def layer_norm(x, g, b):
    xf = x.astype(jnp.float32)
    mu = jnp.mean(xf, axis=-1, keepdims=True)
    var = jnp.mean(jnp.square(xf - mu), axis=-1, keepdims=True)
    return ((xf - mu) * lax.rsqrt(var + LN_EPS) * g + b).astype(x.dtype)


def rms_norm(x, g):
    xf = x.astype(jnp.float32)
    return (xf * lax.rsqrt(jnp.mean(jnp.square(xf), axis=-1, keepdims=True) + RMS_EPS) * g).astype(x.dtype)


def rope(x, pos):
    half = x.shape[-1] // 2
    inv = ROPE_THETA ** (-jnp.arange(half, dtype=jnp.float32) / half)
    ang = pos.astype(jnp.float32)[:, None] * inv[None, :]
    cos, sin = jnp.cos(ang), jnp.sin(ang)
    x1 = x[..., :half].astype(jnp.float32)
    x2 = x[..., half:].astype(jnp.float32)
    return jnp.concatenate([x1 * cos - x2 * sin, x2 * cos + x1 * sin], axis=-1).astype(x.dtype)


def _heads(t, n_heads):
    B, S, _ = t.shape
    return t.reshape(B, S, n_heads, -1).transpose(0, 2, 1, 3)


def _merge(t):
    B, H, S, d = t.shape
    return t.transpose(0, 2, 1, 3).reshape(B, S, H * d)


def over_query_blocks(block_fn, *qs):
    B, H, S, _ = qs[0].shape
    nb = S // QBLOCK
    blks = tuple(q.reshape(B, H, nb, QBLOCK, q.shape[-1]).transpose(2, 0, 1, 3, 4) for q in qs)
    starts = jnp.arange(nb, dtype=jnp.int32) * QBLOCK
    out = lax.map(lambda a: block_fn(a[0] + jnp.arange(QBLOCK, dtype=jnp.int32), *a[1:]), (starts,) + blks)
    return out.transpose(1, 2, 0, 3, 4).reshape(B, H, S, out.shape[-1])


def stick_breaking_attention(q, k, v):
    S = k.shape[2]
    kpos = jnp.arange(S, dtype=jnp.int32)
    scale = HEAD_DIM ** -0.5

    def block(qpos, qb):
        z = jnp.einsum('bhqd,bhkd->bhqk', qb, k).astype(jnp.float32) * scale
        past = kpos[None, :] < qpos[:, None]
        log_beta = jax.nn.log_sigmoid(z)
        log_1m = jnp.where(past, log_beta - z, 0.0)
        between = lax.cumsum(log_1m, axis=3, reverse=True) - log_1m
        w = jnp.where(past, jnp.exp(log_beta + between), 0.0)
        return jnp.einsum('bhqk,bhkd->bhqd', w.astype(v.dtype), v)

    return over_query_blocks(block, q)


def _dilated_branch(q, k, v, span, dil):
    B, H, S, dh = q.shape
    L = S // dil
    nb = -(-L // QBLOCK)
    Lp = nb * QBLOCK

    def strided(t):
        t = t.reshape(B, H, L, dil, dh).transpose(0, 1, 3, 2, 4)
        t = jnp.pad(t, ((0, 0), (0, 0), (0, 0), (0, Lp - L), (0, 0)))
        return t.reshape(B, H, dil, nb, QBLOCK, dh)

    def with_prev(t):
        prev = jnp.pad(t[:, :, :, :-1], ((0, 0), (0, 0), (0, 0), (1, 0), (0, 0), (0, 0)))
        return jnp.concatenate([prev, t], axis=4)

    qs = strided(q)
    kk = with_prev(strided(k))
    vv = with_prev(strided(v))
    s = jnp.einsum('bhrnqd,bhrnkd->bhrnqk', qs, kk).astype(jnp.float32) * HEAD_DIM ** -0.5
    dist = jnp.arange(QBLOCK)[:, None] + QBLOCK - jnp.arange(2 * QBLOCK)[None, :]
    in_band = (dist >= 0) & (dist <= span)
    missing = (jnp.arange(nb)[:, None, None] == 0) & (jnp.arange(2 * QBLOCK) < QBLOCK)[None, None, :]
    valid = in_band[None] & jnp.logical_not(missing)
    s = jnp.where(valid, s, -jnp.inf)
    m = jnp.max(s, axis=-1, keepdims=True)
    e = jnp.exp(s - m)
    den = jnp.sum(e, axis=-1)
    o = jnp.einsum('bhrnqk,bhrnkd->bhrnqd', e, vv.astype(jnp.float32)) / den[..., None]
    lse = m[..., 0] + jnp.log(den)
    o = o.reshape(B, H, dil, Lp, dh)[:, :, :, :L].transpose(0, 1, 3, 2, 4).reshape(B, H, S, dh)
    lse = lse.reshape(B, H, dil, Lp)[..., :L].transpose(0, 1, 3, 2).reshape(B, H, S)
    return o, lse


def dilated_attention(q, k, v):
    outs, lses = [], []
    for window, dil in DIL_BRANCHES:
        o, lse = _dilated_branch(q, k, v, window // dil, dil)
        outs.append(o)
        lses.append(lse)
    w = jax.nn.softmax(jnp.stack(lses), axis=0)
    return jnp.sum(w[..., None] * jnp.stack(outs), axis=0).astype(q.dtype)


def causal_softmax_attention(q, k, v, scale):
    kpos = jnp.arange(k.shape[2], dtype=jnp.int32)

    def block(qpos, qb):
        s = jnp.einsum('bhqd,bhkd->bhqk', qb, k).astype(jnp.float32) * scale
        p = jax.nn.softmax(jnp.where(kpos[None, :] <= qpos[:, None], s, -jnp.inf), axis=-1)
        return jnp.einsum('bhqk,bhkd->bhqd', p.astype(v.dtype), v)

    return over_query_blocks(block, q)


def mla(c_q, c_kv, k_rope_in, pos, q_norm_g, w_uq, kv_norm_g, w_ukv):
    B, S, _ = c_q.shape
    q = _heads(rms_norm(c_q, q_norm_g) @ w_uq, MLA_HEADS)
    q = jnp.concatenate([q[..., :MLA_NOPE], rope(q[..., MLA_NOPE:], pos)], axis=-1)
    kv = _heads(rms_norm(c_kv, kv_norm_g) @ w_ukv, MLA_HEADS)
    k_nope, v = kv[..., :MLA_NOPE], kv[..., MLA_NOPE:]
    k_rope = jnp.broadcast_to(rope(k_rope_in, pos)[:, None], (B, MLA_HEADS, S, MLA_ROPE))
    k = jnp.concatenate([k_nope, k_rope], axis=-1)
    return causal_softmax_attention(q, k, v, (MLA_NOPE + MLA_ROPE) ** -0.5)


def diff_attention(q, k, v, pos, lam_params, subln_g, lam_init):
    B, S, _ = q.shape

    def pair(t):
        t = t.reshape(B, S, DIFF_HEADS, 2, DIFF_DIM).transpose(0, 3, 2, 1, 4)
        return rope(t[:, 0], pos), rope(t[:, 1], pos)

    q1, q2 = pair(q)
    k1, k2 = pair(k)
    vh = _heads(v, DIFF_HEADS)
    lp = lam_params.astype(jnp.float32)
    lam = jnp.exp(jnp.sum(lp[0] * lp[1])) - jnp.exp(jnp.sum(lp[2] * lp[3])) + lam_init
    scale = DIFF_DIM ** -0.5
    kpos = jnp.arange(S, dtype=jnp.int32)

    def block(qpos, q1b, q2b):
        causal = kpos[None, :] <= qpos[:, None]

        def probs(qb, kb):
            s = jnp.einsum('bhqd,bhkd->bhqk', qb, kb).astype(jnp.float32) * scale
            return jax.nn.softmax(jnp.where(causal, s, -jnp.inf), axis=-1)

        a = probs(q1b, k1) - lam * probs(q2b, k2)
        return jnp.einsum('bhqk,bhkd->bhqd', a.astype(vh.dtype), vh)

    o = over_query_blocks(block, q1, q2)
    return rms_norm(o, subln_g) * (1.0 - lam_init)


def hybrid_layer(x, p_i, layer_idx, w_in, w_o, mla_q_norm, mla_w_uq, mla_kv_norm, mla_w_ukv,
                 diff_lambda, diff_subln, ln_attn_g, ln_attn_b, w_ff1, w_ff2, ln_ff_g, ln_ff_b,
                 w_ple_gate, w_ple_proj, ln_ple_g, ln_ple_b):
    S = x.shape[1]
    pos = jnp.arange(S, dtype=jnp.int32)
    h = x @ w_in
    cuts, acc = [], 0
    for wdt in IN_WIDTHS[:-1]:
        acc += wdt
        cuts.append(acc)
    (a_q, a_k, a_v, b_q, b_k, b_v, c_q, c_kv, c_kr, d_q, d_k, d_v) = jnp.split(h, cuts, axis=-1)

    y_a = stick_breaking_attention(_heads(a_q, SB_HEADS), _heads(a_k, SB_HEADS), _heads(a_v, SB_HEADS))
    y_b = dilated_attention(rope(_heads(b_q, DIL_HEADS), pos), rope(_heads(b_k, DIL_HEADS), pos), _heads(b_v, DIL_HEADS))
    y_c = mla(c_q, c_kv, c_kr, pos, mla_q_norm, mla_w_uq, mla_kv_norm, mla_w_ukv)
    lam_init = 0.8 - 0.6 * float(np.exp(-0.3 * layer_idx))
    y_d = diff_attention(d_q, d_k, d_v, pos, diff_lambda, diff_subln, lam_init)

    mix = jnp.concatenate([_merge(y_a), _merge(y_b), _merge(y_c), _merge(y_d)], axis=-1) @ w_o
    x = layer_norm(DN_ALPHA * x + mix, ln_attn_g, ln_attn_b)

    f = jnp.square(jax.nn.relu(x @ w_ff1)) @ w_ff2
    x = layer_norm(DN_ALPHA * x + f, ln_ff_g, ln_ff_b)

    e = jax.nn.sigmoid(x @ w_ple_gate) * (p_i @ w_ple_proj)
    return layer_norm(DN_ALPHA * x + e, ln_ple_g, ln_ple_b)


def setup_inputs(seed: int = 0) -> dict:
    key = jax.random.key(seed)
    ks = jax.random.split(key, 20)
    f32 = jnp.float32

    def nrm(k, shape, scale):
        return jax.random.normal(k, shape, f32) * scale

    def gain(k, shape):
        return 1.0 + 0.02 * jax.random.normal(k, shape, f32)

    L = DEPTH
    return {
        'x': nrm(ks[0], (BATCH, SEQ, D_MODEL), 1.0),
        'p': nrm(ks[1], (DEPTH, BATCH, SEQ, PLE_DIM), 1.0),
        'w_in': nrm(ks[2], (L, D_MODEL, N_IN), D_MODEL ** -0.5),
        'w_o': nrm(ks[3], (L, MIX_OUT, D_MODEL), DN_BETA * MIX_OUT ** -0.5),
        'mla_q_norm': gain(ks[4], (L, MLA_Q_RANK)),
        'mla_w_uq': nrm(ks[5], (L, MLA_Q_RANK, MLA_HEADS * (MLA_NOPE + MLA_ROPE)), MLA_Q_RANK ** -0.5),
        'mla_kv_norm': gain(ks[6], (L, MLA_KV_RANK)),
        'mla_w_ukv': nrm(ks[7], (L, MLA_KV_RANK, MLA_HEADS * (MLA_NOPE + MLA_V)), MLA_KV_RANK ** -0.5),
        'diff_lambda': nrm(ks[8], (L, 4, DIFF_DIM), 0.1),
        'diff_subln': gain(ks[9], (L, 2 * DIFF_DIM)),
        'ln_attn_g': gain(ks[10], (L, D_MODEL)),
        'ln_attn_b': nrm(ks[11], (L, D_MODEL), 0.02),
        'w_ff1': nrm(ks[12], (L, D_MODEL, D_FF), D_MODEL ** -0.5),
        'w_ff2': nrm(ks[13], (L, D_FF, D_MODEL), DN_BETA * D_FF ** -0.5),
        'ln_ff_g': gain(ks[14], (L, D_MODEL)),
        'ln_ff_b': nrm(ks[15], (L, D_MODEL), 0.02),
        'w_ple_gate': nrm(ks[16], (L, D_MODEL, D_MODEL), D_MODEL ** -0.5),
        'w_ple_proj': nrm(ks[17], (L, PLE_DIM, D_MODEL), DN_BETA * PLE_DIM ** -0.5),
        'ln_ple_g': gain(ks[18], (L, D_MODEL)),
        'ln_ple_b': nrm(ks[19], (L, D_MODEL), 0.02),
    }


def reference(x, p, w_in, w_o, mla_q_norm, mla_w_uq, mla_kv_norm, mla_w_ukv, diff_lambda, diff_subln,
              ln_attn_g, ln_attn_b, w_ff1, w_ff2, ln_ff_g, ln_ff_b, w_ple_gate, w_ple_proj, ln_ple_g, ln_ple_b):
    for i in range(DEPTH):
        x = hybrid_layer(x, p[i], i, w_in[i], w_o[i], mla_q_norm[i], mla_w_uq[i], mla_kv_norm[i], mla_w_ukv[i],
                         diff_lambda[i], diff_subln[i], ln_attn_g[i], ln_attn_b[i], w_ff1[i], w_ff2[i],
                         ln_ff_g[i], ln_ff_b[i], w_ple_gate[i], w_ple_proj[i], ln_ple_g[i], ln_ple_b[i])
    return x
```

```python
import numpy as np
from contextlib import ExitStack
import ml_dtypes
import concourse.bass as bass
import concourse.mybir as mybir
from concourse.bass_utils import run_bass_kernel_spmd

F32 = mybir.dt.float32
BF16 = mybir.dt.bfloat16
AF = mybir.ActivationFunctionType
ALU = mybir.AluOpType

D_MODEL = 4096
BATCH = 2
SEQ = 8192
DEPTH = 2
NCORES = 8
D_FF = 16384
PLE_DIM = 256
LN_EPS = 1e-5
RMS_EPS = 1e-6
DN_ALPHA = (2 * DEPTH) ** 0.25
ROPE_THETA = 10000.0
KC = D_MODEL // 128


class Op:
    __slots__ = ("eng", "fn", "r", "w", "dma")

    def __init__(self, eng, fn, r, w, dma):
        self.eng, self.fn, self.r, self.w, self.dma = eng, fn, r, w, dma


class Prog:
    ENGS = ("pe", "act", "dve", "pool", "sp")
    RING = {"sp": 24, "pool": 24, "act": 8}
    CH = 20000

    def __init__(self, nc, es):
        self.nc, self.es = nc, es
        self.ops = []

    def sb(self, name, shape, dt):
        return self.es.enter_context(self.nc.sbuf_tensor(name, list(shape), dt))

    def ps(self, name, shape=(128, 512), dt=F32):
        return self.es.enter_context(self.nc.psum_tensor(name, list(shape), dt))

    PSK = ("pmm", "ph", "pr", "pss", "ps_s", "ps_q", "pS", "pO", "pZ", "pE")

    def add(self, eng, fn, r=(), w=()):
        r = tuple(r)
        w = tuple(w) + tuple(k for k in r if (k[0] if isinstance(k, tuple) else k) in self.PSK)
        self.ops.append(Op(eng, fn, r, w, False))

    def dma(self, q, out, in_, r=(), w=()):
        self.ops.append(Op(q, (lambda e, o=out, i=in_: e.dma_start(out=o, in_=i)), tuple(r), tuple(w), True))

    def emit(self):
        nc, ops = self.nc, self.ops
        n = len(ops)
        pos = [0] * n
        cnt = {e: 0 for e in self.ENGS}
        per_eng = {e: [] for e in self.ENGS}
        for i, o in enumerate(ops):
            pos[i] = cnt[o.eng]
            cnt[o.eng] += 1
            per_eng[o.eng].append(i)
        last_w = {}
        readers = {}
        seen = {e: {p: -1 for p in self.ENGS} for e in self.ENGS}
        seen_dma = {e: set() for e in self.ENGS}
        waits = [None] * n
        signal = [False] * n
        for i, o in enumerate(ops):
            deps = set()
            for k in o.r:
                d = last_w.get(k)
                if d is not None:
                    deps.add(d)
            for k in o.w:
                d = last_w.get(k)
                if d is not None:
                    deps.add(d)
                rd = readers.get(k)
                if rd:
                    deps.update(rd[0].values())
                    deps.update(rd[1])
            wl = []
            for d in sorted(deps):
                p = ops[d]
                if p.fn is None:
                    assert p.eng == o.eng
                    continue
                if p.dma:
                    if d not in seen_dma[o.eng]:
                        seen_dma[o.eng].add(d)
                        wl.append(d)
                elif p.eng == o.eng and not o.dma:
                    if o.eng == "pe":
                        continue
                    if pos[i] - pos[d] <= 2:
                        wl.append(d)
                        signal[d] = True
                else:
                    if pos[d] > seen[o.eng][p.eng]:
                        seen[o.eng][p.eng] = pos[d]
                        wl.append(d)
                        signal[d] = True
            waits[i] = wl
            for k in o.r:
                rd = readers.get(k)
                if rd is None:
                    rd = readers[k] = ({}, [])
                if o.dma:
                    rd[1].append(i)
                else:
                    rd[0][o.eng] = i
            for k in o.w:
                last_w[k] = i
                readers[k] = ({}, [])
        sig_idx = [0] * n
        nsig = {e: 0 for e in self.ENGS}
        dma_idx = [0] * n
        ndma = {e: 0 for e in self.ENGS}
        for i, o in enumerate(ops):
            if o.dma:
                dma_idx[i] = ndma[o.eng]
                ndma[o.eng] += 1
            elif signal[i]:
                nsig[o.eng] += 1
                sig_idx[i] = nsig[o.eng]
        es = self.es
        csem = {e: [es.enter_context(nc.semaphore(f"c_{e}_{j}")) for j in range((nsig[e] + self.CH - 1) // self.CH)]
                for e in self.ENGS}
        dsem = {e: [es.enter_context(nc.semaphore(f"d_{e}_{j}")) for j in range(min(self.RING.get(e, 0), ndma[e]))]
                for e in self.ENGS}

        def wait_for(e, d):
            p = ops[d]
            if p.dma:
                R = len(dsem[p.eng])
                m = dma_idx[d]
                e.wait_ge(dsem[p.eng][m % R], 16 * (m // R + 1))
            else:
                c = sig_idx[d] - 1
                e.wait_ge(csem[p.eng][c // self.CH], c % self.CH + 1)

        def mk(en):
            def body(e):
                for i in per_eng[en]:
                    o = ops[i]
                    for d in waits[i]:
                        wait_for(e, d)
                    if o.fn is None:
                        continue
                    if o.dma:
                        R = len(dsem[en])
                        m = dma_idx[i]
                        if m >= R:
                            e.wait_ge(dsem[en][m % R], 16 * (m // R))
                        o.fn(e).then_inc(dsem[en][m % R], 16)
                    else:
                        ins = o.fn(e)
                        if signal[i]:
                            c = sig_idx[i] - 1
                            ins.then_inc(csem[en][c // self.CH], 1)
            return body

        with nc.Block() as block:
            block.tensor(mk("pe"))
            block.scalar(mk("act"))
            block.vector(mk("dve"))
            block.gpsimd(mk("pool"))
            block.sync(mk("sp"))


def phase_r(pg, NT, d):
    T = 512
    NTT = NT // T
    acc = pg.sb("r_acc", (128, KC, T), F32)
    xb = pg.sb("r_xb", (128, KC, T), BF16)
    ring = [pg.sb(f"r_w{i}", (128, 16384), BF16) for i in range(2)]
    h1 = [pg.sb(f"r_h1_{i}", (128, 4, T), BF16) for i in range(2)]
    wpc = [pg.sb(f"r_wp{i}", (128, 2, 512), BF16) for i in range(2)]
    pt = pg.sb("r_pt", (128, 2, T), BF16)
    lnp = pg.sb("r_lnp", (128, 6 * KC), F32)
    lna = pg.sb("r_lna", (128, 4 * KC), F32)
    ones = pg.sb("r_ones", (128, 128), BF16)
    scr_b = [pg.sb(f"r_scrb{i}", (128, T), BF16) for i in range(4)]
    scr_f = [pg.sb(f"r_scrf{i}", (128, T), F32) for i in range(4)]
    st_mean = pg.sb("r_mean", (128, T), F32)
    st_rstd = pg.sb("r_rstd", (128, T), F32)
    st_mr = pg.sb("r_mr", (128, T), F32)
    st_tmp = pg.sb("r_sttmp", (128, T), F32)
    ps_mm = [pg.ps(f"r_pmm{i}") for i in range(4)]
    ps_h = [pg.ps(f"r_ph{i}") for i in range(2)]
    ps_s = pg.ps("r_pss")
    ps_q = pg.ps("r_psq")

    pg.dma("sp", lnp[:], d["lnp"][:, :], w=["lnp"])
    pg.add("dve", lambda e: e.memset(ones[:], 1.0), w=["ones"])
    pg.add("dve", lambda e: e.tensor_scalar(out=lna[:, 0:2 * KC], in0=lnp[:, 0:2 * KC], scalar1=float(DN_ALPHA), scalar2=None, op0=ALU.mult),
           r=["lnp"], w=["lna0"])
    pg.add("dve", lambda e: e.tensor_scalar(out=lna[:, 2 * KC:4 * KC], in0=lnp[:, 2 * KC:4 * KC], scalar1=float(DN_ALPHA), scalar2=None, op0=ALU.mult),
           r=["lnp"], w=["lna1"])

    wcount = [0]

    def load_w(src_ap, width):
        i = wcount[0] % 2
        wcount[0] += 1
        key = f"ring{i}"
        pg.dma("pool", ring[i][:, 0:width].rearrange("p (a b) -> p a b", a=src_ap.shape[1]), src_ap, w=[key])
        return ring[i], key

    mmc = [0]

    def layer_norm(t, li, prescale):
        for c in range(KC):
            sb_ = scr_b[c % 2]
            sq_ = scr_b[2 + c % 2]
            pg.add("act", lambda e, c=c, sb_=sb_: e.activation(out=sb_[:], in_=acc[:, c, :], func=AF.Copy),
                   r=[("acc", c)], w=[("scrb", c % 2)])
            pg.add("act", lambda e, c=c, sq_=sq_: e.activation(out=sq_[:], in_=acc[:, c, :], func=AF.Square),
                   r=[("acc", c)], w=[("scrb", 2 + c % 2)])
            pg.add("pe", lambda e, c=c, sb_=sb_: e.matmul(ps_s[:], ones[:], sb_[:], start=(c == 0), stop=(c == KC - 1)),
                   r=[("scrb", c % 2), "ones"], w=["ps_s"])
            pg.add("pe", lambda e, c=c, sq_=sq_: e.matmul(ps_q[:], ones[:], sq_[:], start=(c == 0), stop=(c == KC - 1)),
                   r=[("scrb", 2 + c % 2), "ones"], w=["ps_q"])
        inv = 1.0 / D_MODEL
        pg.add("dve", lambda e: e.tensor_scalar(out=st_mean[:], in0=ps_s[:], scalar1=inv, scalar2=None, op0=ALU.mult),
               r=["ps_s"], w=["st_mean"])
        pg.add("dve", lambda e: e.tensor_tensor(out=st_tmp[:], in0=st_mean[:], in1=st_mean[:], op=ALU.mult),
               r=["st_mean"], w=["st_tmp"])
        pg.add("dve", lambda e: e.scalar_tensor_tensor(out=st_rstd[:], in0=ps_q[:], scalar=inv, in1=st_tmp[:], op0=ALU.mult, op1=ALU.subtract),
               r=["ps_q", "st_tmp"], w=["st_rstd"])
        pg.add("act", lambda e: e.activation(out=st_tmp[:], in_=st_rstd[:], func=AF.Ln, bias=float(LN_EPS)),
               r=["st_rstd"], w=["st_tmp"])
        pg.add("act", lambda e: e.activation(out=st_rstd[:], in_=st_tmp[:], func=AF.Exp, scale=-0.5),
               r=["st_tmp"], w=["st_rstd"])
        pg.add("dve", lambda e: e.tensor_tensor(out=st_mr[:], in0=st_mean[:], in1=st_rstd[:], op=ALU.mult),
               r=["st_mean", "st_rstd"], w=["st_mr"])
        g0 = (2 * li) * KC
        b0 = (2 * li + 1) * KC
        for c in range(KC):
            f_ = scr_f[c % 2]
            f2 = scr_f[2 + c % 2]
            pg.add("dve", lambda e, c=c, f_=f_: e.tensor_tensor(out=f_[:], in0=acc[:, c, :], in1=st_rstd[:], op=ALU.mult),
                   r=[("acc", c), "st_rstd"], w=[("scrf", c % 2)])
            pg.add("dve", lambda e, c=c, f_=f_, f2=f2: e.tensor_tensor(out=f2[:], in0=f_[:], in1=st_mr[:], op=ALU.subtract),
                   r=[("scrf", c % 2), "st_mr"], w=[("scrf", 2 + c % 2)])
            pg.add("act", lambda e, c=c, f2=f2: e.activation(out=xb[:, c, :], in_=f2[:], func=AF.Identity,
                                                          scale=lnp[:, g0 + c:g0 + c + 1], bias=lnp[:, b0 + c:b0 + c + 1]),
                   r=[("scrf", 2 + c % 2), "lnp"], w=[("xb", c)])
            if prescale:
                pg.add("act", lambda e, c=c, f2=f2: e.activation(out=acc[:, c, :], in_=f2[:], func=AF.Identity,
                                                              scale=lna[:, g0 + c:g0 + c + 1], bias=lna[:, b0 + c:b0 + c + 1]),
                       r=[("scrf", 2 + c % 2), "lna0", "lna1"], w=[("acc", c)])
            else:
                pg.add("act", lambda e, c=c, f2=f2: e.activation(out=acc[:, c, :], in_=f2[:], func=AF.Identity,
                                                              scale=lnp[:, g0 + c:g0 + c + 1], bias=lnp[:, b0 + c:b0 + c + 1]),
                       r=[("scrf", 2 + c % 2), "lnp"], w=[("acc", c)])

    def proj_cols(src_w, evac, pre_ct=None):
        for ct in range(8):
            if pre_ct is not None:
                pre_ct(ct)
            wt, wkey = load_w(src_w[:, ct * 512:(ct + 1) * 512].rearrange("(k p) c -> p k c", p=128), 16384)
            for j in range(4):
                oc = ct * 4 + j
                pi = mmc[0] % 4
                mmc[0] += 1
                pt_ = ps_mm[pi]

                def grp(e, wt=wt, j=j, pt_=pt_):
                    ins = None
                    for k in range(KC):
                        ins = e.matmul(pt_[:], wt[:, k * 512 + j * 128:k * 512 + (j + 1) * 128], xb[:, k, :],
                                       start=(k == 0), stop=(k == KC - 1))
                    return ins
                pg.add("pe", grp, r=[wkey] + [("xb", k) for k in range(KC)], w=[("pmm", pi)])
                evac(oc, pt_, ("pmm", pi))

    for t in range(NTT):
        ts = slice(t * T, (t + 1) * T)
        pg.dma("sp", xb[:], d["yT"][:, ts].rearrange("(k p) t -> p k t", p=128), w=[("xb", k) for k in range(KC)])
        pg.dma("sp", acc[:], d["xT"][:, ts].rearrange("(k p) t -> p k t", p=128), w=[("acc", k) for k in range(KC)])
        pg.dma("pool", pt[:], d["pT"][:, ts].rearrange("(k p) t -> p k t", p=128), w=["pt"])

        def evac_wo(oc, pt_, pkey):
            pg.add("dve", lambda e: e.scalar_tensor_tensor(out=acc[:, oc, :], in0=acc[:, oc, :], scalar=float(DN_ALPHA), in1=pt_[:],
                                                           op0=ALU.mult, op1=ALU.add),
                   r=[("acc", oc), pkey], w=[("acc", oc)])
        proj_cols(d["w_o"], evac_wo)
        layer_norm(t, 0, True)

        for grp_i in range(D_FF // 512):
            w1, w1key = load_w(d["w_ff1"][:, grp_i * 512:(grp_i + 1) * 512].rearrange("(k p) c -> p k c", p=128), 16384)
            hb = h1[grp_i % 2]
            for j in range(4):
                pi = (grp_i * 4 + j) % 2
                ph = ps_h[pi]

                def grp1(e, w1=w1, j=j, ph=ph):
                    ins = None
                    for k in range(KC):
                        ins = e.matmul(ph[:], w1[:, k * 512 + j * 128:k * 512 + (j + 1) * 128], xb[:, k, :],
                                       start=(k == 0), stop=(k == KC - 1))
                    return ins
                pg.add("pe", grp1, r=[w1key] + [("xb", k) for k in range(KC)], w=[("ph", pi)])
                rf = scr_f[pi]
                pg.add("act", lambda e, ph=ph, rf=rf: e.activation(out=rf[:], in_=ph[:], func=AF.Relu),
                       r=[("ph", pi)], w=[("scrf", pi)])
                pg.add("dve", lambda e, rf=rf, hb=hb, j=j: e.tensor_tensor(out=hb[:, j, :], in0=rf[:], in1=rf[:], op=ALU.mult),
                       r=[("scrf", pi)], w=[("h1", grp_i % 2, j)])
            w2, w2key = load_w(d["w_ff2"][grp_i * 512:(grp_i + 1) * 512, :].rearrange("(j p) c -> p j c", p=128), 16384)
            for oc in range(KC):
                pi = mmc[0] % 4
                mmc[0] += 1
                pt_ = ps_mm[pi]

                def grp2(e, w2=w2, oc=oc, pt_=pt_, hb=hb):
                    ins = None
                    for j in range(4):
                        ins = e.matmul(pt_[:], w2[:, j * 4096 + oc * 128:j * 4096 + (oc + 1) * 128], hb[:, j, :],
                                       start=(j == 0), stop=(j == 3))
                    return ins
                pg.add("pe", grp2, r=[w2key] + [("h1", grp_i % 2, j) for j in range(4)], w=[("pmm", pi)])
                pg.add("dve", lambda e, oc=oc, pt_=pt_: e.tensor_tensor(out=acc[:, oc, :], in0=acc[:, oc, :], in1=pt_[:], op=ALU.add),
                       r=[("acc", oc), ("pmm", pi)], w=[("acc", oc)])
        layer_norm(t, 1, True)

        def pre_ple(ct):
            pg.dma("pool", wpc[ct % 2][:], d["w_proj"][:, ct * 512:(ct + 1) * 512].rearrange("(k p) c -> p k c", p=128), w=[("wp", ct % 2)])

        def evac_ple(oc, pt_, pkey):
            pi2 = oc % 2
            ph = ps_h[pi2]
            wp = wpc[(oc // 4) % 2]
            j4 = oc % 4

            def grpp(e, ph=ph, wp=wp, j4=j4):
                ins = None
                for k in range(2):
                    ins = e.matmul(ph[:], wp[:, k, j4 * 128:(j4 + 1) * 128], pt[:, k, :], start=(k == 0), stop=(k == 1))
                return ins
            pg.add("pe", grpp, r=[("wp", (oc // 4) % 2), "pt"], w=[("ph", pi2)])
            sf = scr_f[pi2]
            pg.add("act", lambda e, pt_=pt_, sf=sf: e.activation(out=sf[:], in_=pt_[:], func=AF.Sigmoid),
                   r=[pkey], w=[("scrf", pi2)])
            sf2 = scr_f[2 + pi2]
            pg.add("dve", lambda e, sf=sf, sf2=sf2, ph=ph: e.tensor_tensor(out=sf2[:], in0=sf[:], in1=ph[:], op=ALU.mult),
                   r=[("scrf", pi2), ("ph", pi2)], w=[("scrf", 2 + pi2)])
            pg.add("dve", lambda e, oc=oc, sf2=sf2: e.tensor_tensor(out=acc[:, oc, :], in0=acc[:, oc, :], in1=sf2[:], op=ALU.add),
                   r=[("acc", oc), ("scrf", 2 + pi2)], w=[("acc", oc)])
        proj_cols(d["w_gate"], evac_ple, pre_ple)
        layer_norm(t, 2, False)
        pg.dma("sp", d["xoT"][:, ts].rearrange("(k p) t -> p k t", p=128), acc[:], r=[("acc", k) for k in range(KC)], w=[("xo", t)])
    pg.add("sp", None, r=[("xo", t) for t in range(NTT)])


def build_r(NT):
    nc = bass.Bass("TRN2", target_bir_lowering=False)
    d = {}
    d["yT"] = nc.dram_tensor("yT", [D_MODEL, NT], BF16, kind="ExternalInput").ap()
    d["xT"] = nc.dram_tensor("xT", [D_MODEL, NT], F32, kind="ExternalInput").ap()
    d["pT"] = nc.dram_tensor("pT", [PLE_DIM, NT], F32, kind="ExternalInput").ap()
    d["w_o"] = nc.dram_tensor("w_o", [D_MODEL, D_MODEL], F32, kind="ExternalInput").ap()
    d["w_ff1"] = nc.dram_tensor("w_ff1", [D_MODEL, D_FF], F32, kind="ExternalInput").ap()
    d["w_ff2"] = nc.dram_tensor("w_ff2", [D_FF, D_MODEL], F32, kind="ExternalInput").ap()
    d["w_gate"] = nc.dram_tensor("w_gate", [D_MODEL, D_MODEL], F32, kind="ExternalInput").ap()
    d["w_proj"] = nc.dram_tensor("w_proj", [PLE_DIM, D_MODEL], F32, kind="ExternalInput").ap()
    d["lnp"] = nc.dram_tensor("lnp", [128, 6 * KC], F32, kind="ExternalInput").ap()
    d["xoT"] = nc.dram_tensor("xoT", [D_MODEL, NT], F32, kind="ExternalOutput").ap()
    with ExitStack() as es:
        pg = Prog(nc, es)
        phase_r(pg, NT, d)
        pg.emit()
    return nc


def vec_pc(v):
    return np.ascontiguousarray(v.reshape(-1, 128).T)


NWC = 3968
NFT = 18
P_TILES = [
    (0, 512, "plain", 0),
    (512, 512, "rope", 4),
    (1024, 512, "rope", 8),
    (1536, 512, "c", 0),
    (2048, 512, "c", 4),
    (2560, 512, "c", 8),
    (3072, 128, "c", 12),
    (3200, 512, "v", 0),
    (3712, 256, "v", 512),
]


def phase_p(pg, S, d, mla=True):
    T = 1024
    NTT = S // T
    xt = pg.sb("p_xt", (128, KC, T), BF16)
    ring = [pg.sb(f"p_w{i}", (128, 16384), BF16) for i in range(2)]
    tab = [pg.sb(f"p_tab{i}", (128, T), F32) for i in range(4)]
    wuq = pg.sb("p_wuq", (128, 8, 384), BF16)
    wukv = pg.sb("p_wukv", (128, 4, 512), BF16)
    nrm = pg.sb("p_nrm", (128, 12), F32)
    rmat = pg.sb("p_rmat", (128, 2, 128), BF16)
    ones = pg.sb("p_ones", (128, 128), BF16)
    so = [pg.sb(f"p_so{i}", (128, T), BF16) for i in range(3)]
    scrf = [pg.sb(f"p_scrf{i}", (128, 512), F32) for i in range(4)]
    hb = [pg.sb(f"p_hb{i}", (128, 512), BF16) for i in range(2)]
    rstd = [pg.sb(f"p_rstd{i}", (128, T), F32) for i in range(2)]
    rtok = pg.sb("p_rtok", (128, 8), F32)
    vst = [pg.sb(f"p_vst{i}", (128, 512), BF16) for i in range(2)]
    ps_mm = [pg.ps(f"p_pmm{i}") for i in range(4)]
    ps_r = [pg.ps(f"p_pr{i}") for i in range(2)]
    ps_s = [pg.ps(f"p_pss{i}") for i in range(2)]

    pg.dma("pool", wuq[:], d["wuq"].rearrange("(k p) c -> p k c", p=128), w=["wuq"])
    pg.dma("pool", wukv[:], d["wukv"].rearrange("(k p) c -> p k c", p=128), w=["wukv"])
    pg.dma("sp", nrm[:, 0:8], d["qn"][:, :], w=["nrm"])
    pg.dma("sp", nrm[:, 8:12], d["kvn"][:, :], w=["nrm"])
    pg.dma("sp", rmat[:], d["rmat"].rearrange("r p c -> p r c"), w=["rmat"])
    pg.add("dve", lambda e: e.memset(ones[:], 1.0), w=["ones"])
    for k in range(8):
        pg.add("dve", lambda e, k=k: e.tensor_scalar(out=wuq[:, k, :], in0=wuq[:, k, :], scalar1=nrm[:, k:k + 1], scalar2=None, op0=ALU.mult),
               r=["wuq", "nrm"], w=["wuq"])
    for k in range(4):
        pg.add("dve", lambda e, k=k: e.tensor_scalar(out=wukv[:, k, :], in0=wukv[:, k, :], scalar1=nrm[:, 8 + k:9 + k], scalar2=None, op0=ALU.mult),
               r=["wukv", "nrm"], w=["wukv"])

    cnt = {"w": 0, "mm": 0, "r": 0, "so": 0, "v": 0, "f": 0, "hb": 0}

    def nxt(name, n):
        i = cnt[name] % n
        cnt[name] += 1
        return i

    def rope_evac(src_ap, src_keys, ti_cos, ti_sin, tcol, rm, out_ap, out_key, src_is_psum=True):
        f1 = nxt("f", 4)
        f2 = nxt("f", 4)
        pr = nxt("r", 2)
        cs = tab[ti_cos][:, tcol]
        sn = tab[ti_sin][:, tcol]
        if src_ap.dtype == BF16:
            hsrc, hkeys = src_ap, list(src_keys)
        else:
            hi = nxt("hb", 2)
            pg.add("act", lambda e: e.activation(out=hb[hi][:], in_=src_ap, func=AF.Copy), r=list(src_keys), w=[("hb", hi)] + list(src_keys))
            hsrc, hkeys = hb[hi][:], [("hb", hi)]
        pg.add("dve", lambda e: e.tensor_tensor(out=scrf[f1][:], in0=src_ap, in1=cs, op=ALU.mult),
               r=list(src_keys) + [("tab", ti_cos)], w=[("scrf", f1)])
        pg.add("pe", lambda e: e.matmul(ps_r[pr][:], rmat[:, rm, :], hsrc, start=True, stop=True),
               r=hkeys + ["rmat"], w=[("pr", pr)])
        pg.add("dve", lambda e: e.tensor_tensor(out=scrf[f2][:], in0=ps_r[pr][:], in1=sn, op=ALU.mult),
               r=[("pr", pr), ("tab", ti_sin)], w=[("scrf", f2)])
        pg.add("dve", lambda e: e.tensor_tensor(out=out_ap, in0=scrf[f1][:], in1=scrf[f2][:], op=ALU.add),
               r=[("scrf", f1), ("scrf", f2)], w=[out_key])

    for tt in range(NTT):
        ts = slice(tt * T, (tt + 1) * T)
        for q4 in range(4):
            pg.dma("pool", xt[:, q4 * 8:(q4 + 1) * 8, :], d["xT"][q4 * 1024:(q4 + 1) * 1024, ts].rearrange("(k p) t -> p k t", p=128),
                   w=[("xt", q4)])
        pg.dma("sp", tab[0][:], d["cs128"][0, :, ts], w=[("tab", 0)])
        pg.dma("sp", tab[1][:], d["cs128"][1, :, ts], w=[("tab", 1)])
        xkeys = [("xt", q4) for q4 in range(4)]
        for (c0, ncol, kind, dst) in P_TILES:
            wi = nxt("w", 2)
            wt = ring[wi]
            pg.dma("pool", wt[:, 0:KC * ncol].rearrange("p (k c) -> p k c", k=KC),
                   d["w"][:, c0:c0 + ncol].rearrange("(k p) c -> p k c", p=128), w=[("ring", wi)])
            if kind == "v":
                for tb in range(T // 128):
                    pi = nxt("mm", 4)
                    pm = ps_mm[pi]

                    def grpv(e, wt=wt, tb=tb, pm=pm, ncol=ncol):
                        ins = None
                        for k in range(KC):
                            ins = e.matmul(pm[:, 0:ncol], xt[:, k, tb * 128:(tb + 1) * 128], wt[:, k * ncol:(k + 1) * ncol],
                                           start=(k == 0), stop=(k == KC - 1))
                        return ins
                    pg.add("pe", grpv, r=[("ring", wi)] + xkeys, w=[("pmm", pi)])
                    vi = nxt("v", 2)
                    pg.add("act", lambda e, pm=pm, vi=vi, ncol=ncol: e.activation(out=vst[vi][:, 0:ncol], in_=pm[:, 0:ncol], func=AF.Copy),
                           r=[("pmm", pi)], w=[("vst", vi)])
                    t0 = tt * T + tb * 128
                    pg.dma("sp", d["VT"][t0:t0 + 128, dst:dst + ncol], vst[vi][:, 0:ncol], r=[("vst", vi)], w=[("VT", tt, tb, dst)])
                continue
            for j in range(ncol // 128):
                si = nxt("so", 3)
                for half in range(2):
                    pi = nxt("mm", 4)
                    pm = ps_mm[pi]
                    hs = slice(half * 512, (half + 1) * 512)

                    def grpf(e, wt=wt, j=j, pm=pm, hs=hs, ncol=ncol):
                        ins = None
                        for k in range(KC):
                            ins = e.matmul(pm[:], wt[:, k * ncol + j * 128:k * ncol + (j + 1) * 128], xt[:, k, hs],
                                           start=(k == 0), stop=(k == KC - 1))
                        return ins
                    pg.add("pe", grpf, r=[("ring", wi)] + xkeys, w=[("pmm", pi)])
                    if kind == "rope":
                        rope_evac(pm[:], [("pmm", pi)], 0, 1, hs, 0, so[si][:, hs], ("so", si, half))
                    else:
                        pg.add("act", lambda e, pm=pm, si=si, hs=hs: e.activation(out=so[si][:, hs], in_=pm[:], func=AF.Copy),
                               r=[("pmm", pi)], w=[("so", si, half)])
                dest = d["CT"][dst + j, :, ts] if kind == "c" else d["FT"][dst + j, :, ts]
                dkey = ("CT" if kind == "c" else "FT", dst + j, tt)
                pg.dma("sp", dest, so[si][:], r=[("so", si, 0), ("so", si, 1)], w=[dkey])

    pg.add("sp", None, w=[("xt", q4) for q4 in range(4)])
    for tt in range(NTT if mla else 0):
        ts = slice(tt * T, (tt + 1) * T)
        for k in range(13):
            pg.dma("sp", xt[:, k, :], d["CT"][k, :, ts], r=[("CT", k, tt)], w=[("mx", k)])
        pg.dma("sp", tab[2][:], d["cs64"][0, :, ts], w=[("tab", 2)])
        pg.dma("sp", tab[3][:], d["cs64"][1, :, ts], w=[("tab", 3)])
        for which, nk, base, N in ((0, 8, 0, 1024.0), (1, 4, 8, 512.0)):
            for half in range(2):
                hs = slice(half * 512, (half + 1) * 512)
                pss = ps_s[half]
                for k in range(nk):
                    hi = nxt("hb", 2)
                    if which == 1:
                        sq_ap, sq_key = xt[:, 13 + k, hs], ("mx", 13 + k, half)
                    else:
                        sq_ap, sq_key = hb[hi][:], ("hb", hi)
                    pg.add("dve", lambda e, sq_ap=sq_ap, k=k, hs=hs, base=base: e.tensor_tensor(out=sq_ap, in0=xt[:, base + k, hs], in1=xt[:, base + k, hs], op=ALU.mult),
                           r=[("mx", base + k)], w=[sq_key])
                    pg.add("pe", lambda e, sq_ap=sq_ap, k=k, nk=nk, pss=pss: e.matmul(pss[:], ones[:], sq_ap, start=(k == 0), stop=(k == nk - 1)),
                           r=[sq_key, "ones"], w=[("pss", half)])
                f1 = nxt("f", 4)
                pg.add("act", lambda e, f1=f1, pss=pss, N=N: e.activation(out=scrf[f1][:], in_=pss[:], func=AF.Ln, scale=1.0 / N, bias=float(RMS_EPS)),
                       r=[("pss", half)], w=[("scrf", f1)])
                pg.add("act", lambda e, f1=f1, which=which, hs=hs: e.activation(out=rstd[which][:, hs], in_=scrf[f1][:], func=AF.Exp, scale=-0.5),
                       r=[("scrf", f1)], w=[("rstd", which, half)])
        for tb in range(T // 128):
            half = tb // 4
            pr = nxt("r", 2)

            def grpt(e, tb=tb, pr=pr):
                ins = None
                for k in range(4):
                    ins = e.matmul(ps_r[pr][:, 0:1], xt[:, 13 + k, tb * 128:(tb + 1) * 128], ones[:, 0:1], start=(k == 0), stop=(k == 3))
                return ins
            pg.add("pe", grpt, r=[("mx", 13 + k, half) for k in range(4)] + ["ones"], w=[("pr", pr)])
            f1 = nxt("f", 4)
            pg.add("act", lambda e, f1=f1, pr=pr: e.activation(out=scrf[f1][:, 0:1], in_=ps_r[pr][:, 0:1], func=AF.Ln, scale=1.0 / 512.0, bias=float(RMS_EPS)),
                   r=[("pr", pr)], w=[("scrf", f1)])
            pg.add("act", lambda e, f1=f1, tb=tb: e.activation(out=rtok[:, tb:tb + 1], in_=scrf[f1][:, 0:1], func=AF.Exp, scale=-0.5),
                   r=[("scrf", f1)], w=[("rtok", tb)])
        for j in range(3):
            si = nxt("so", 3)
            for half in range(2):
                hs = slice(half * 512, (half + 1) * 512)
                pi = nxt("mm", 4)
                pm = ps_mm[pi]

                def grpq(e, j=j, pm=pm, hs=hs):
                    ins = None
                    for k in range(8):
                        ins = e.matmul(pm[:], wuq[:, k, j * 128:(j + 1) * 128], xt[:, k, hs], start=(k == 0), stop=(k == 7))
                    return ins
                pg.add("pe", grpq, r=["wuq"] + [("mx", k) for k in range(8)], w=[("pmm", pi)])
                if j < 2:
                    pg.add("dve", lambda e, pm=pm, si=si, hs=hs: e.tensor_tensor(out=so[si][:, hs], in0=pm[:], in1=rstd[0][:, hs], op=ALU.mult),
                           r=[("pmm", pi), ("rstd", 0, half)], w=[("so", si, half)])
                else:
                    f0 = nxt("f", 4)
                    pg.add("dve", lambda e, pm=pm, f0=f0, hs=hs: e.tensor_tensor(out=scrf[f0][:], in0=pm[:], in1=rstd[0][:, hs], op=ALU.mult),
                           r=[("pmm", pi), ("rstd", 0, half)], w=[("scrf", f0)])
                    rope_evac(scrf[f0][:], [("scrf", f0)], 2, 3, hs, 1, so[si][:, hs], ("so", si, half))
            pg.dma("sp", d["FT"][12 + j, :, ts], so[si][:], r=[("so", si, 0), ("so", si, 1)], w=[("FT", 12 + j, tt)])
        for j in range(2):
            si = nxt("so", 3)
            for half in range(2):
                hs = slice(half * 512, (half + 1) * 512)
                pi = nxt("mm", 4)
                pm = ps_mm[pi]

                def grpk(e, j=j, pm=pm, hs=hs):
                    ins = None
                    for k in range(4):
                        ins = e.matmul(pm[:], wukv[:, k, j * 128:(j + 1) * 128], xt[:, 8 + k, hs], start=(k == 0), stop=(k == 3))
                    return ins
                pg.add("pe", grpk, r=["wukv"] + [("mx", 8 + k) for k in range(4)], w=[("pmm", pi)])
                pg.add("dve", lambda e, pm=pm, si=si, hs=hs: e.tensor_tensor(out=so[si][:, hs], in0=pm[:], in1=rstd[1][:, hs], op=ALU.mult),
                       r=[("pmm", pi), ("rstd", 1, half)], w=[("so", si, half)])
            pg.dma("sp", d["FT"][15 + j, :, ts], so[si][:], r=[("so", si, 0), ("so", si, 1)], w=[("FT", 15 + j, tt)])
        for tb in range(T // 128):
            pi = nxt("mm", 4)
            pm = ps_mm[pi]

            def grpvc(e, tb=tb, pm=pm):
                ins = None
                for k in range(4):
                    ins = e.matmul(pm[:, 0:256], xt[:, 8 + k, tb * 128:(tb + 1) * 128], wukv[:, k, 256:512], start=(k == 0), stop=(k == 3))
                return ins
            pg.add("pe", grpvc, r=["wukv"] + [("mx", 8 + k) for k in range(4)], w=[("pmm", pi)])
            vi = nxt("v", 2)
            pg.add("dve", lambda e, pm=pm, vi=vi, tb=tb: e.tensor_scalar(out=vst[vi][:, 0:256], in0=pm[:, 0:256], scalar1=rtok[:, tb:tb + 1], scalar2=None, op0=ALU.mult),
                   r=[("pmm", pi), ("rtok", tb)], w=[("vst", vi)])
            t0 = tt * T + tb * 128
            pg.dma("sp", d["VT"][t0:t0 + 128, 768:1024], vst[vi][:, 0:256], r=[("vst", vi)], w=[("VT", tt, tb, 768)])
        si = nxt("so", 3)
        for half in range(2):
            hs = slice(half * 512, (half + 1) * 512)
            rope_evac(xt[:, 12, hs], [("mx", 12)], 2, 3, hs, 1, so[si][:, hs], ("so", si, half))
        pg.dma("sp", d["FT"][17, :, ts], so[si][:], r=[("so", si, 0), ("so", si, 1)], w=[("FT", 17, tt)])
    pg.add("sp", None, r=[k for o in pg.ops for k in o.w if isinstance(k, tuple) and k[0] in ("FT", "VT", "CT")])
    return
    fin = [("FT", c, tt) for c in range(NFT) for tt in range(NTT)] + [("VT", tt, tb, dst) for tt in range(NTT) for tb in range(T // 128) for dst in (0, 512, 768)]
    fin += [("CT", c, tt) for c in range(13) for tt in range(NTT)]
    pg.add("sp", None, r=fin)


def build_p(S, mla=True):
    nc = bass.Bass("TRN2", target_bir_lowering=False)
    d = {}
    d["xT"] = nc.dram_tensor("xT", [D_MODEL, S], F32, kind="ExternalInput").ap()
    d["w"] = nc.dram_tensor("w", [D_MODEL, NWC], F32, kind="ExternalInput").ap()
    d["wuq"] = nc.dram_tensor("wuq", [1024, 384], F32, kind="ExternalInput").ap()
    d["wukv"] = nc.dram_tensor("wukv", [512, 512], F32, kind="ExternalInput").ap()
    d["qn"] = nc.dram_tensor("qn", [128, 8], F32, kind="ExternalInput").ap()
    d["kvn"] = nc.dram_tensor("kvn", [128, 4], F32, kind="ExternalInput").ap()
    d["cs128"] = nc.dram_tensor("cs128", [2, 128, S], F32, kind="ExternalInput").ap()
    d["cs64"] = nc.dram_tensor("cs64", [2, 128, S], F32, kind="ExternalInput").ap()
    d["rmat"] = nc.dram_tensor("rmat", [2, 128, 128], BF16, kind="ExternalInput").ap()
    d["FT"] = nc.dram_tensor("FT", [NFT, 128, S], BF16, kind="ExternalOutput").ap()
    d["VT"] = nc.dram_tensor("VT", [S, 1024], BF16, kind="ExternalOutput").ap()
    d["CT"] = nc.dram_tensor("CT", [13, 128, S], BF16, kind="ExternalOutput").ap()
    with ExitStack() as es:
        pg = Prog(nc, es)
        phase_p(pg, S, d, mla)
        pg.emit()
    return nc


_OFF = {}
_acc = 0
for _n, _w in (("aq", 1024), ("ak", 1024), ("av", 1024), ("bq", 1024), ("bk", 1024), ("bv", 1024),
               ("cq", 1024), ("ckv", 512), ("kr", 64), ("dq", 1024), ("dk", 1024), ("dv", 1024)):
    _OFF[_n] = _acc
    _acc += _w
N_IN = _acc


def w_in_cols(g):
    r = lambda name, a, n: list(range(_OFF[name] + a, _OFF[name] + a + n))
    cols = []
    cols += r("aq", 256 * g, 256) + r("ak", 256 * g, 256)
    cols += r("bq", 256 * g, 256) + r("bk", 256 * g, 256)
    cols += r("dq", 256 * g, 256) + r("dk", 256 * g, 256)
    cols += r("cq", 0, 1024) + r("ckv", 0, 512) + r("kr", 0, 64) + r("kr", 0, 64)
    cols += r("av", 256 * g, 256) + r("bv", 256 * g, 256) + r("dv", 256 * g, 256)
    assert len(cols) == NWC
    return np.array(cols)


def wuq_cols(g):
    c = []
    for h in (2 * g, 2 * g + 1):
        c += list(range(h * 192, h * 192 + 128))
    for h in (2 * g, 2 * g + 1):
        c += list(range(h * 192 + 128, h * 192 + 192))
    return np.array(c)


def wukv_cols(g):
    c = []
    for h in (2 * g, 2 * g + 1):
        c += list(range(h * 256, h * 256 + 128))
    for h in (2 * g, 2 * g + 1):
        c += list(range(h * 256 + 128, h * 256 + 256))
    return np.array(c)


def rope_tables(S):
    pos = np.arange(S, dtype=np.float32)
    out = []
    for half, reps in ((64, 1), (32, 2)):
        inv = (ROPE_THETA ** (-np.arange(half, dtype=np.float32) / half)).astype(np.float32)
        ang = pos[None, :] * inv[:, None]
        cos = np.cos(ang).astype(np.float32)
        sin = np.sin(ang).astype(np.float32)
        c2 = np.concatenate([cos, cos] * reps, axis=0)
        s2 = np.concatenate([sin, sin] * reps, axis=0)
        out.append(np.ascontiguousarray(np.stack([c2, s2])))
    return out[0], out[1]


def rot_mats():
    m = np.zeros((2, 128, 128), np.float32)
    for mm in range(128):
        if mm < 64:
            m[0, mm + 64, mm] = -1.0
        else:
            m[0, mm - 64, mm] = 1.0
    for blk in range(2):
        for mp in range(64):
            mm = blk * 64 + mp
            if mp < 32:
                m[1, mm + 32, mm] = -1.0
            else:
                m[1, mm - 32, mm] = 1.0
    return m.astype(ml_dtypes.bfloat16)


def phase_a(pg, S, d, maps=None):
    NQ = S // 512
    NB = S // 128
    kt = [pg.sb(f"a_kt{i}", (128, S), BF16) for i in range(2)]
    qt = [pg.sb(f"a_qt{i}", (128, S), BF16) for i in range(2)]
    vt = pg.sb("a_vt", (128, NB, 256), BF16)
    mask_c = pg.sb("a_mc", (128, 4, 512), BF16)
    mask_s = pg.sb("a_ms", (128, 4, 512), BF16)
    mask_b = pg.sb("a_mb", (128, 20, 512), BF16)
    tex = pg.sb("a_tex", (128, 128), BF16)
    ones = pg.sb("a_ones", (128, 128), BF16)
    pb = [pg.sb(f"a_pb{i}", (128, 512), BF16) for i in range(4)]
    ef = [pg.sb(f"a_e{i}", (128, 512), F32) for i in range(2)]
    spb = [pg.sb(f"a_sp{i}", (128, 512), BF16) for i in range(2)]
    tmpf = [pg.sb(f"a_tmp{i}", (128, 512), F32) for i in range(2)]
    invb = [pg.sb(f"a_ib{i}", (128, 512), F32) for i in range(2)]
    ggf = [pg.sb(f"a_gg{i}", (128, 512), F32) for i in range(2)]
    Rf = pg.sb("a_R", (128, 512), F32)
    o1 = pg.sb("a_o1", (128, 2, 512), F32)
    rz = pg.sb("a_rz", (128, 512), F32)
    tf = [pg.sb(f"a_tf{i}", (128, 512), F32) for i in range(2)]
    yst = [pg.sb(f"a_y{i}", (128, 512), BF16) for i in range(4)]
    lam_sb = pg.sb("a_lam", (128, 512), F32)
    lamw = pg.sb("a_lamw", (128, 8), F32)
    lamc = pg.sb("a_lamc", (128, 2), F32)
    gsub = pg.sb("a_gsub", (128, 2), F32)
    pS = [pg.ps(f"a_pS{i}") for i in range(2)]
    pE = [pg.ps(f"a_pE{i}") for i in range(2)]
    pO = [pg.ps(f"a_pO{i}") for i in range(2)]
    pZ = pg.ps("a_pZ")

    pg.dma("sp", mask_c[:], d["mask_c"].rearrange("m p c -> p m c"), w=["mask_c"])
    pg.dma("sp", mask_s[:], d["mask_s"].rearrange("m p c -> p m c"), w=["mask_s"])
    pg.dma("sp", mask_b[:], d["mask_b"].rearrange("m p c -> p m c"), w=["mask_b"])
    pg.dma("sp", tex[:], d["tex"][:, :], w=["tex"])
    pg.dma("sp", lam_sb[:], d["lamp"].partition_broadcast(128), w=["lam_sb"])
    pg.dma("sp", lamc[:], d["lamc"][:, :], w=["lamc"])
    pg.dma("sp", gsub[:], d["subln"][:, :], w=["gsub"])
    pg.add("dve", lambda e: e.memset(ones[:], 1.0), w=["ones"])
    pg.add("dve", lambda e: e.tensor_tensor(out=lam_sb[:, 0:128], in0=lam_sb[:, 0:128], in1=lam_sb[:, 128:256], op=ALU.mult), r=["lam_sb"], w=["lam_a"])
    pg.add("dve", lambda e: e.tensor_tensor(out=lam_sb[:, 256:384], in0=lam_sb[:, 256:384], in1=lam_sb[:, 384:512], op=ALU.mult), r=["lam_sb"], w=["lam_b"])
    pg.add("dve", lambda e: e.reduce_sum(out=lamw[:, 0:1], in_=lam_sb[:, 0:128], axis=mybir.AxisListType.X), r=["lam_a"], w=["lamw0"])
    pg.add("dve", lambda e: e.reduce_sum(out=lamw[:, 1:2], in_=lam_sb[:, 256:384], axis=mybir.AxisListType.X), r=["lam_b"], w=["lamw1"])
    pg.add("act", lambda e: e.activation(out=lamw[:, 2:4], in_=lamw[:, 0:2], func=AF.Exp), r=["lamw0", "lamw1"], w=["lamw2"])
    pg.add("dve", lambda e: e.tensor_tensor(out=lamw[:, 4:5], in0=lamw[:, 3:4], in1=lamw[:, 2:3], op=ALU.subtract), r=["lamw2"], w=["lamw4"])
    pg.add("dve", lambda e: e.tensor_tensor(out=lamw[:, 5:6], in0=lamw[:, 4:5], in1=lamc[:, 0:1], op=ALU.subtract), r=["lamw4", "lamc"], w=["neglam"])
    pg.add("dve", lambda e: e.tensor_scalar(out=gsub[:], in0=gsub[:], scalar1=lamc[:, 1:2], scalar2=None, op0=ALU.mult), r=["gsub", "lamc"], w=["gsub"])
    neglam = lamw[:, 5:6]

    cnt = {"S": 0, "pb": 0, "y": 0, "e": 0, "sp": 0, "tmp": 0, "ib": 0, "gg": 0, "tf": 0}

    def nxt(name, n):
        i = cnt[name] % n
        cnt[name] += 1
        return i

    def load_map(kparts, qparts, vcol, dv):
        for i, c in enumerate(kparts):
            pg.dma("sp", kt[i][:], d["FT"][c, :, :], w=[("kt", i)])
        for i, c in enumerate(qparts):
            pg.dma("sp", qt[i][:], d["FT"][c, :, :], w=[("qt", i)])
        if vcol is not None:
            pg.dma("sp", vt[:, :, 0:dv], d["VT"][:, vcol:vcol + dv].rearrange("(n p) c -> p n c", p=128), w=["vt"])

    def qk(g, kb, parts):
        si = nxt("S", 2)

        def f(e):
            ins = None
            for n, (ki, qi, p0, p1) in enumerate(parts):
                ins = e.matmul(pS[si][:], kt[ki][p0:p1, kb * 128:(kb + 1) * 128], qt[qi][p0:p1, g * 512:(g + 1) * 512],
                               start=(n == 0), stop=(n == len(parts) - 1))
            return ins
        pg.add("pe", f, r=[("kt", p[0]) for p in parts] + [("qt", p[1]) for p in parts], w=[("pS", si)])
        return si

    def softmax_tile(g, parts, scale, nh, band):
        if band:
            kbs = list(range(4 * g + 3, max(-1, 4 * g - 17), -1))
        else:
            kbs = list(range(4 * g + 3, -1, -1))
        for n, kb in enumerate(kbs):
            si = qk(g, kb, parts)
            pi = nxt("pb", 4)
            pg.add("act", lambda e, si=si, pi=pi: e.activation(out=pb[pi][:], in_=pS[si][:], func=AF.Exp, scale=float(scale)),
                   r=[("pS", si)], w=[("pb", pi)])
            if band:
                dl = 4 * g - kb + 3
                pg.add("dve", lambda e, pi=pi, dl=dl: e.tensor_tensor(out=pb[pi][:], in0=pb[pi][:], in1=mask_b[:, dl, :], op=ALU.mult),
                       r=[("pb", pi), "mask_b"], w=[("pb", pi)])
            elif kb >= 4 * g:
                m = kb - 4 * g
                pg.add("dve", lambda e, pi=pi, m=m: e.tensor_tensor(out=pb[pi][:], in0=pb[pi][:], in1=mask_c[:, m, :], op=ALU.mult),
                       r=[("pb", pi), "mask_c"], w=[("pb", pi)])
            first, last = (n == 0), (n == len(kbs) - 1)

            def f(e, kb=kb, pi=pi, first=first, last=last):
                for h in range(nh):
                    e.matmul(pO[h][:], vt[:, kb, h * 128:(h + 1) * 128], pb[pi][:], start=first, stop=last)
                return e.matmul(pZ[:], ones[:], pb[pi][:], start=first, stop=last)
            pg.add("pe", f, r=[("pb", pi), "vt", "ones"], w=[("pO", h) for h in range(nh)] + ["pZ"])

    def store_y(chunk, g, yi):
        pg.dma("sp", d["yT"][chunk, :, g * 512:(g + 1) * 512], yst[yi][:], r=[("y", yi)], w=[("yT", chunk, g)])

    def run_softmax_map(parts, scale, band, ychunk):
        for g in range(NQ):
            softmax_tile(g, parts, scale, 1, band)
            pg.add("dve", lambda e: e.reciprocal(out=rz[:], in_=pZ[:]), r=["pZ"], w=["rz"])
            yi = nxt("y", 4)
            pg.add("dve", lambda e, yi=yi: e.tensor_tensor(out=yst[yi][:], in0=pO[0][:], in1=rz[:], op=ALU.mult),
                   r=[("pO", 0), "rz"], w=[("y", yi)])
            store_y(ychunk, g, yi)

    def run_sb_map(ychunk):
        scale = 128 ** -0.5
        for g in range(NQ):
            kbs = list(range(4 * g + 3, -1, -1))
            pg.add("dve", lambda e: e.memset(Rf[:], 0.0), w=["R"])
            for n, kb in enumerate(kbs):
                si = qk(g, kb, [(0, 0, 0, 128)])
                ei, spi, ti, ii, gi, wi = nxt("e", 2), nxt("sp", 2), nxt("tmp", 2), nxt("ib", 2), nxt("gg", 2), nxt("pb", 4)
                diag = kb >= 4 * g
                m = kb - 4 * g
                pg.add("act", lambda e, si=si, ei=ei: e.activation(out=ef[ei][:], in_=pS[si][:], func=AF.Exp, scale=float(scale)),
                       r=[("pS", si)], w=[("e", ei)])
                pg.add("act", lambda e, ei=ei, spi=spi: e.activation(out=spb[spi][:], in_=ef[ei][:], func=AF.Ln, bias=1.0),
                       r=[("e", ei)], w=[("sp", spi)])
                if diag:
                    pg.add("dve", lambda e, spi=spi, m=m: e.tensor_tensor(out=spb[spi][:], in0=spb[spi][:], in1=mask_s[:, m, :], op=ALU.mult),
                           r=[("sp", spi), "mask_s"], w=[("sp", spi)])

                def fc(e, spi=spi, kb=kb, g=g):
                    e.matmul(pE[0][:], tex[:], spb[spi][:], start=True, stop=False)
                    e.matmul(pE[0][:], kt[1][:, kb * 128:(kb + 1) * 128], qt[0][:, g * 512:(g + 1) * 512], start=False, stop=True)
                    return e.matmul(pE[1][:], ones[:], spb[spi][:], start=True, stop=True)
                pg.add("pe", fc, r=[("sp", spi), "tex", "ones", ("kt", 1), ("qt", 0)], w=[("pE", 0), ("pE", 1)])
                pg.add("dve", lambda e, ti=ti: e.tensor_tensor(out=tmpf[ti][:], in0=pE[0][:], in1=Rf[:], op=ALU.add),
                       r=[("pE", 0), "R"], w=[("tmp", ti)])
                pg.add("dve", lambda e: e.tensor_tensor(out=Rf[:], in0=pE[1][:], in1=Rf[:], op=ALU.add),
                       r=[("pE", 1), "R"], w=["R"])
                pg.add("act", lambda e, ti=ti, wi=wi: e.activation(out=pb[wi][:], in_=tmpf[ti][:], func=AF.Exp, scale=-1.0),
                       r=[("tmp", ti)], w=[("pb", wi)])
                if diag:
                    pg.add("dve", lambda e, wi=wi, m=m: e.tensor_tensor(out=pb[wi][:], in0=pb[wi][:], in1=mask_s[:, m, :], op=ALU.mult),
                           r=[("pb", wi), "mask_s"], w=[("pb", wi)])
                first, last = (n == 0), (n == len(kbs) - 1)
                pg.add("pe", lambda e, kb=kb, wi=wi, first=first, last=last: e.matmul(pO[0][:], vt[:, kb, 0:128], pb[wi][:], start=first, stop=last),
                       r=[("pb", wi), "vt"], w=[("pO", 0)])
            yi = nxt("y", 4)
            pg.add("act", lambda e, yi=yi: e.activation(out=yst[yi][:], in_=pO[0][:], func=AF.Copy), r=[("pO", 0)], w=[("y", yi)])
            store_y(ychunk, g, yi)

    def run_diff(ychunk):
        scale = 128 ** -0.5
        for g in range(NQ):
            for mp in range(2):
                softmax_tile(g, [(mp, mp, 0, 128)], scale, 2, False)
                pg.add("dve", lambda e: e.reciprocal(out=rz[:], in_=pZ[:]), r=["pZ"], w=["rz"])
                for h in range(2):
                    if mp == 0:
                        pg.add("dve", lambda e, h=h: e.tensor_tensor(out=o1[:, h, :], in0=pO[h][:], in1=rz[:], op=ALU.mult),
                               r=[("pO", h), "rz"], w=[("o1", h)])
                    else:
                        ti = nxt("tf", 2)
                        pg.add("dve", lambda e, h=h, ti=ti: e.tensor_tensor(out=tf[ti][:], in0=pO[h][:], in1=rz[:], op=ALU.mult),
                               r=[("pO", h), "rz"], w=[("tf", ti)])
                        pg.add("dve", lambda e, h=h, ti=ti: e.scalar_tensor_tensor(out=o1[:, h, :], in0=tf[ti][:], scalar=neglam, in1=o1[:, h, :],
                                                                                 op0=ALU.mult, op1=ALU.add),
                               r=[("tf", ti), ("o1", h), "neglam"], w=[("o1", h)])
            for h in range(2):
                pi = nxt("pb", 4)
                pg.add("act", lambda e, h=h, pi=pi: e.activation(out=pb[pi][:], in_=o1[:, h, :], func=AF.Square), r=[("o1", h)], w=[("pb", pi)])
                pg.add("pe", lambda e, h=h, pi=pi: e.matmul(pZ[:], ones[:], pb[pi][:], start=(h == 0), stop=(h == 1)),
                       r=[("pb", pi), "ones"], w=["pZ"])
            ti = nxt("tf", 2)
            pg.add("act", lambda e, ti=ti: e.activation(out=tf[ti][:], in_=pZ[:], func=AF.Ln, scale=1.0 / 256.0, bias=float(RMS_EPS)), r=["pZ"], w=[("tf", ti)])
            pg.add("act", lambda e, ti=ti: e.activation(out=rz[:], in_=tf[ti][:], func=AF.Exp, scale=-0.5), r=[("tf", ti)], w=["rz"])
            for h in range(2):
                yi = nxt("y", 4)
                pg.add("dve", lambda e, h=h, yi=yi: e.scalar_tensor_tensor(out=yst[yi][:], in0=o1[:, h, :], scalar=gsub[:, h:h + 1], in1=rz[:],
                                                                         op0=ALU.mult, op1=ALU.mult),
                       r=[("o1", h), "gsub", "rz"], w=[("y", yi)])
                store_y(ychunk + h, g, yi)

    allmaps = maps if maps is not None else ["A0", "A1", "B0", "B1", "C0", "C1", "D"]
    for mname in allmaps:
        kind, hh = mname[0], (int(mname[1]) if len(mname) > 1 else 0)
        if kind == "A":
            load_map([2 + hh], [0 + hh], 128 * hh, 128)
            pg.add("dve", lambda e: e.tensor_scalar(out=kt[1][:], in0=kt[0][:], scalar1=-(128 ** -0.5), scalar2=None, op0=ALU.mult),
                   r=[("kt", 0)], w=[("kt", 1)])
            run_sb_map(0 + hh)
        elif kind == "B":
            load_map([6 + hh], [4 + hh], 256 + 128 * hh, 128)
            run_softmax_map([(0, 0, 0, 128)], 128 ** -0.5, True, 2 + hh)
        elif kind == "C":
            load_map([15 + hh, 17], [12 + hh, 14], 768 + 128 * hh, 128)
            run_softmax_map([(0, 0, 0, 128), (1, 1, 64 * hh, 64 * hh + 64)], 192 ** -0.5, False, 4 + hh)
        else:
            load_map([10, 11], [8, 9], 512, 256)
            run_diff(6)
    pg.add("sp", None, r=[k for o in pg.ops for k in o.w if isinstance(k, tuple) and k[0] == "yT"])


def build_a(S, maps=None):
    nc = bass.Bass("TRN2", target_bir_lowering=False)
    d = {}
    d["FT"] = nc.dram_tensor("FT", [NFT, 128, S], BF16, kind="ExternalInput").ap()
    d["VT"] = nc.dram_tensor("VT", [S, 1024], BF16, kind="ExternalInput").ap()
    d["mask_c"] = nc.dram_tensor("mask_c", [4, 128, 512], BF16, kind="ExternalInput").ap()
    d["mask_s"] = nc.dram_tensor("mask_s", [4, 128, 512], BF16, kind="ExternalInput").ap()
    d["mask_b"] = nc.dram_tensor("mask_b", [20, 128, 512], BF16, kind="ExternalInput").ap()
    d["tex"] = nc.dram_tensor("tex", [128, 128], BF16, kind="ExternalInput").ap()
    d["lamp"] = nc.dram_tensor("lamp", [512], F32, kind="ExternalInput").ap()
    d["lamc"] = nc.dram_tensor("lamc", [128, 2], F32, kind="ExternalInput").ap()
    d["subln"] = nc.dram_tensor("subln", [128, 2], F32, kind="ExternalInput").ap()
    d["yT"] = nc.dram_tensor("yT", [8, 128, S], BF16, kind="ExternalOutput").ap()
    with ExitStack() as es:
        pg = Prog(nc, es)
        phase_a(pg, S, d, maps)
        pg.emit()
    return nc


def attn_consts():
    i = np.arange(128)[:, None]
    j = np.arange(512)[None, :]
    mc = np.stack([(i + 128 * m <= j) for m in range(4)]).astype(np.float32)
    ms = np.stack([(i + 128 * m < j) for m in range(4)]).astype(np.float32)
    mb = []
    for dl in range(-3, 17):
        dd = 128 * dl + j - i
        c = ((dd >= 0) & (dd <= 128)).astype(np.float32)
        c += ((dd >= 0) & (dd % 4 == 0) & (dd <= 512)).astype(np.float32)
        c += ((dd >= 0) & (dd % 16 == 0) & (dd <= 2048)).astype(np.float32)
        mb.append(c)
    mb = np.stack(mb)
    k = np.arange(128)
    tex = (k[:, None] >= k[None, :]).astype(np.float32)
    bf = ml_dtypes.bfloat16
    return dict(mask_c=mc.astype(bf), mask_s=ms.astype(bf), mask_b=mb.astype(bf), tex=tex.astype(bf))


_CACHE = {}


def _get(name, fn):
    if name not in _CACHE:
        _CACHE[name] = fn()
    return _CACHE[name]


def _run(nc, in_maps):
    return run_bass_kernel_spmd(nc, in_maps, core_ids=list(range(NCORES))).results


def kernel(x, p, w_in, w_o, mla_q_norm, mla_w_uq, mla_kv_norm, mla_w_ukv, diff_lambda, diff_subln,
           ln_attn_g, ln_attn_b, w_ff1, w_ff2, ln_ff_g, ln_ff_b, w_ple_gate, w_ple_proj, ln_ple_g, ln_ple_b):
    f32 = np.float32
    A = lambda a: np.asarray(a)
    x, p = A(x), A(p)
    S = SEQ
    NT = S // 4
    ncP = _get("P", lambda: build_p(S))
    ncA = _get("A", lambda: build_a(S))
    ncR = _get("R", lambda: build_r(NT))
    cs128, cs64 = rope_tables(S)
    rmat = rot_mats()
    ac = attn_consts()
    xT = [np.ascontiguousarray(x[b].T) for b in range(BATCH)]
    for l in range(DEPTH):
        wl = A(w_in[l])
        maps = []
        for c in range(NCORES):
            b, g = divmod(c, 4)
            maps.append(dict(xT=xT[b], w=np.ascontiguousarray(wl[:, w_in_cols(g)]),
                             wuq=np.ascontiguousarray(A(mla_w_uq[l])[:, wuq_cols(g)]),
                             wukv=np.ascontiguousarray(A(mla_w_ukv[l])[:, wukv_cols(g)]),
                             qn=vec_pc(A(mla_q_norm[l])), kvn=vec_pc(A(mla_kv_norm[l])),
                             cs128=cs128, cs64=cs64, rmat=rmat))
        resP = _run(ncP, maps)
        lam_init = 0.8 - 0.6 * float(np.exp(-0.3 * l))
        lamc = np.tile(np.array([[lam_init, 1.0 - lam_init]], f32), (128, 1))
        maps = []
        for c in range(NCORES):
            maps.append(dict(FT=resP[c]["FT"], VT=resP[c]["VT"], lamp=np.ascontiguousarray(A(diff_lambda[l]).reshape(-1)),
                             lamc=lamc, subln=vec_pc(A(diff_subln[l])), **ac))
        resA = _run(ncA, maps)
        del resP
        yfull = []
        for b in range(BATCH):
            yf = np.empty((D_MODEL, S), dtype=ml_dtypes.bfloat16)
            for g in range(4):
                y = resA[b * 4 + g]["yT"]
                for fam in range(4):
                    yf[1024 * fam + 256 * g:1024 * fam + 256 * g + 256] = y[2 * fam:2 * fam + 2].reshape(256, S)
            yfull.append(yf)
        del resA
        lnp = np.ascontiguousarray(np.concatenate([vec_pc(A(v[l])) for v in (ln_attn_g, ln_attn_b, ln_ff_g, ln_ff_b, ln_ple_g, ln_ple_b)], axis=1))
        maps = []
        for c in range(NCORES):
            b, j = divmod(c, 4)
            ts = slice(j * NT, (j + 1) * NT)
            maps.append(dict(yT=np.ascontiguousarray(yfull[b][:, ts]), xT=np.ascontiguousarray(xT[b][:, ts]),
                             pT=np.ascontiguousarray(p[l, b, ts].T), w_o=A(w_o[l]), w_ff1=A(w_ff1[l]), w_ff2=A(w_ff2[l]),
                             w_gate=A(w_ple_gate[l]), w_proj=A(w_ple_proj[l]), lnp=lnp))
        resR = _run(ncR, maps)
        xT = [np.ascontiguousarray(np.concatenate([resR[b * 4 + j]["xoT"] for j in range(4)], axis=1)) for b in range(BATCH)]
        del resR
    return np.ascontiguousarray(np.stack([xT[b].T for b in range(BATCH)])).astype(f32)
```
